# Optimizing a Trainium2 kernel written in Bass

```python
import jax, jax.numpy as jnp
from jax import lax
import numpy as np

D_MODEL = 4096
BATCH = 4
SEQ = 4096
DEPTH = 2
DEC_BATCH = 8
DEC_SEQ = 64
PAST_LEN = 2048

CHUNK = 64
D_MIX = D_MODEL
D_POOL = D_MIX // 4
D_SCONV = (D_MIX - D_POOL) // 2
D_CCONV = D_MIX - D_POOL - D_SCONV
POOL_WINDOWS = (2, 4, 8, 16)
N_POOL_GROUPS = len(POOL_WINDOWS)
POOL_GROUP = D_POOL // N_POOL_GROUPS
POOL_HIST = max(POOL_WINDOWS) - 1
SCONV_WIDTH = 3
CCONV_WIDTH = 31
D_IN = D_POOL + 3 * D_SCONV + 2 * D_CCONV
D_FF = -(-8 * D_MODEL // (3 * 256)) * 256
RMS_EPS = 1e-6
LN_EPS = 1e-5

kernel_name = "hymba_pool_conv_conformer_stream_step"


def rmsnorm(x, g):
    xf = x.astype(jnp.float32)
    y = xf * lax.rsqrt(jnp.mean(xf * xf, axis=-1, keepdims=True) + RMS_EPS)
    return (y * g.astype(jnp.float32)).astype(x.dtype)


def layernorm(x, g, b):
    xf = x.astype(jnp.float32)
    mu = jnp.mean(xf, axis=-1, keepdims=True)
    var = jnp.mean(jnp.square(xf - mu), axis=-1, keepdims=True)
    y = (xf - mu) * lax.rsqrt(var + LN_EPS)
    return (y * g.astype(jnp.float32) + b.astype(jnp.float32)).astype(x.dtype)


def causal_depthwise_conv(u, hist, w):
    k = w.shape[0]
    p = jnp.concatenate([hist.astype(u.dtype), u], axis=1)
    y = lax.conv_general_dilated(
        p, w[:, None, :].astype(u.dtype), window_strides=(1,), padding='VALID',
        dimension_numbers=('NWC', 'WIO', 'NWC'), feature_group_count=u.shape[-1])
    return y, p[:, p.shape[1] - (k - 1):]


def pool_mixer(v, hist, pos0, w_grp, scale):
    bsz, t = v.shape[0], v.shape[1]
    p = jnp.concatenate([hist.astype(v.dtype), v], axis=1)
    cs = jnp.cumsum(p.astype(jnp.float32), axis=1)
    cs = jnp.pad(cs, ((0, 0), (1, 0), (0, 0)))
    end = cs[:, POOL_HIST + 1:]
    pos = pos0 + jnp.arange(t)
    means = []
    for g, k in enumerate(POOL_WINDOWS):
        lo, hi = g * POOL_GROUP, (g + 1) * POOL_GROUP
        start = cs[:, POOL_HIST + 1 - k: POOL_HIST + 1 - k + t, lo:hi]
        cnt = jnp.minimum(k, pos + 1).astype(jnp.float32)[None, :, None]
        means.append((end[..., lo:hi] - start) / cnt)
    mean = jnp.stack(means, axis=2)
    d = mean - v.astype(jnp.float32).reshape(bsz, t, N_POOL_GROUPS, POOL_GROUP)
    y = jnp.einsum('btgc,gcd->btgd', d.astype(v.dtype), w_grp)
    return y.reshape(bsz, t, D_POOL) * scale, p[:, p.shape[1] - POOL_HIST:]


def mixer_block(h, hist_pool, hist_sconv, hist_cconv, pos0, w_in, pool_w, pool_scale,
                sconv_w, cconv_w, cconv_b, cnorm_g, cnorm_b, w_out):
    u = jnp.einsum('btd,de->bte', h, w_in)
    i1 = D_POOL
    i2 = i1 + D_SCONV
    i3 = i2 + D_SCONV
    i4 = i3 + D_SCONV
    i5 = i4 + D_CCONV
    v_a = u[..., :i1]
    gate_b, gate_c, x_b = u[..., i1:i2], u[..., i2:i3], u[..., i3:i4]
    c_val, c_gate = u[..., i4:i5], u[..., i5:]
    y_a, new_pool = pool_mixer(v_a, hist_pool, pos0, pool_w, pool_scale)
    z_b, new_sconv = causal_depthwise_conv(gate_c * x_b, hist_sconv, sconv_w)
    y_b = gate_b * z_b
    glu = c_val * jax.nn.sigmoid(c_gate)
    z_c, new_cconv = causal_depthwise_conv(glu, hist_cconv, cconv_w)
    y_c = jax.nn.silu(layernorm(z_c + cconv_b, cnorm_g, cnorm_b))
    y = jnp.concatenate([y_a, y_b, y_c], axis=-1)
    return jnp.einsum('bte,ed->btd', y, w_out), new_pool, new_sconv, new_cconv


def swiglu(h, w_gate, w_up, w_down):
    a = jnp.einsum('btd,df->btf', h, w_gate)
    b = jnp.einsum('btd,df->btf', h, w_up)
    return jnp.einsum('btf,fd->btd', jax.nn.silu(a) * b, w_down)


def trunk(x, hist_pool, hist_sconv, hist_cconv, pos0, norm_mix, norm_ffn, w_in, pool_w,
          pool_scale, sconv_w, cconv_w, cconv_b, cnorm_g, cnorm_b, w_out, w_gate, w_up,
          w_down, norm_final):
    pools, sconvs, cconvs = [], [], []
    for l in range(DEPTH):
        m, sp, ss, sc = mixer_block(rmsnorm(x, norm_mix[l]), hist_pool[l], hist_sconv[l],
                                    hist_cconv[l], pos0, w_in[l], pool_w[l], pool_scale[l],
                                    sconv_w[l], cconv_w[l], cconv_b[l], cnorm_g[l], cnorm_b[l],
                                    w_out[l])
        x = x + m
        x = x + swiglu(rmsnorm(x, norm_ffn[l]), w_gate[l], w_up[l], w_down[l])
        pools.append(sp)
        sconvs.append(ss)
        cconvs.append(sc)
    return rmsnorm(x, norm_final), jnp.stack(pools), jnp.stack(sconvs), jnp.stack(cconvs)


def setup_inputs(seed: int = 0) -> dict:
    key = jax.random.key(seed)
    ks = jax.random.split(key, 24)
    f32 = jnp.float32
    nrm = lambda k, s, sc: jax.random.normal(k, s, f32) * sc
    return {
        "x_prompt": nrm(ks[0], (BATCH, SEQ, D_MODEL), 1.0),
        "x_sample": nrm(ks[1], (DEC_BATCH, DEC_SEQ, D_MODEL), 1.0),
        "cache_pool": nrm(ks[2], (DEPTH, DEC_BATCH, POOL_HIST, D_POOL), 1.0),
        "cache_sconv": nrm(ks[3], (DEPTH, DEC_BATCH, SCONV_WIDTH - 1, D_SCONV), 1.0),
        "cache_cconv": nrm(ks[4], (DEPTH, DEC_BATCH, CCONV_WIDTH - 1, D_CCONV), 1.0),
        "norm_mix": 1.0 + nrm(ks[5], (DEPTH, D_MODEL), 0.1),
        "norm_ffn": 1.0 + nrm(ks[6], (DEPTH, D_MODEL), 0.1),
        "w_in": nrm(ks[7], (DEPTH, D_MODEL, D_IN), D_MODEL ** -0.5),
        "pool_w": nrm(ks[8], (DEPTH, N_POOL_GROUPS, POOL_GROUP, POOL_GROUP), POOL_GROUP ** -0.5),
        "pool_scale": 0.5 + nrm(ks[9], (DEPTH, D_POOL), 0.05),
        "sconv_w": nrm(ks[10], (DEPTH, SCONV_WIDTH, D_SCONV), SCONV_WIDTH ** -0.5),
        "cconv_w": nrm(ks[11], (DEPTH, CCONV_WIDTH, D_CCONV), CCONV_WIDTH ** -0.5),
        "cconv_b": nrm(ks[12], (DEPTH, D_CCONV), 0.01),
        "cnorm_g": 1.0 + nrm(ks[13], (DEPTH, D_CCONV), 0.1),
        "cnorm_b": nrm(ks[14], (DEPTH, D_CCONV), 0.01),
        "w_out": nrm(ks[15], (DEPTH, D_MIX, D_MODEL), D_MIX ** -0.5),
        "w_gate": nrm(ks[16], (DEPTH, D_MODEL, D_FF), D_MODEL ** -0.5),
        "w_up": nrm(ks[17], (DEPTH, D_MODEL, D_FF), D_MODEL ** -0.5),
        "w_down": nrm(ks[18], (DEPTH, D_FF, D_MODEL), D_FF ** -0.5),
        "norm_final": 1.0 + nrm(ks[19], (D_MODEL,), 0.1),
    }


def reference(x_prompt, x_sample, cache_pool, cache_sconv, cache_cconv, norm_mix, norm_ffn,
              w_in, pool_w, pool_scale, sconv_w, cconv_w, cconv_b, cnorm_g, cnorm_b, w_out,
              w_gate, w_up, w_down, norm_final):
    dt = x_prompt.dtype
    zero_pool = jnp.zeros((DEPTH, BATCH, POOL_HIST, D_POOL), dt)
    zero_sconv = jnp.zeros((DEPTH, BATCH, SCONV_WIDTH - 1, D_SCONV), dt)
    zero_cconv = jnp.zeros((DEPTH, BATCH, CCONV_WIDTH - 1, D_CCONV), dt)
    y_prompt, pool_prompt, sconv_prompt, cconv_prompt = trunk(
        x_prompt, zero_pool, zero_sconv, zero_cconv, 0, norm_mix, norm_ffn, w_in, pool_w,
        pool_scale, sconv_w, cconv_w, cconv_b, cnorm_g, cnorm_b, w_out, w_gate, w_up, w_down,
        norm_final)
    y_sample, pool_sample, sconv_sample, cconv_sample = trunk(
        x_sample, cache_pool, cache_sconv, cache_cconv, PAST_LEN, norm_mix, norm_ffn, w_in,
        pool_w, pool_scale, sconv_w, cconv_w, cconv_b, cnorm_g, cnorm_b, w_out, w_gate, w_up,
        w_down, norm_final)
    return (y_prompt, y_sample, pool_prompt, pool_sample, sconv_prompt, sconv_sample,
            cconv_prompt, cconv_sample)
```

```python
import os
import contextlib
import numpy as np
import concourse.bass as bass
import concourse.mybir as mybir
from concourse.bass_utils import run_bass_kernel_spmd

F32 = mybir.dt.float32
BF16 = mybir.dt.bfloat16
AF = mybir.ActivationFunctionType
ALU = mybir.AluOpType

ENGS = ["pe", "act", "dve", "pool", "sp"]
POOL_WINDOWS = (2, 4, 8, 16)
HP, HS, HC = 15, 2, 30
KS, KC = 3, 31
RMS_EPS = 1e-6
LN_EPS = 1e-5
HALO = 64
NSLOT = 4
NRING = 6
NSCR = int(os.environ.get('K_NSCR', '6'))
NBF = int(os.environ.get('K_NBF', '4'))
import os
SCONV_SHIFT = int(os.environ.get('K_SHIFT', '2'))
FENCE_ALL = os.environ.get('K_FENCEALL', '0') == '1'
PREFETCH = os.environ.get('K_PREFETCH', '1') == '1'
INC_STATS = os.environ.get('K_STATS', '1') == '1'


class Cfg:
    def __init__(self, D, PG, nS, nC, F, L, Lm, Ls, tiles, parts):
        self.D, self.PG, self.nS, self.nC, self.F, self.L = D, PG, nS, nC, F, L
        self.KD = D // 128
        self.PGc = PG // 128
        self.KP = 4 * self.PGc
        self.KM = self.KP + nS + nC
        self.NF = F // 128
        self.NJ = self.KP + 3 * nS + 2 * nC
        self.Lm, self.Ls = Lm, Ls
        self.T = HALO + Lm + Ls
        self.tiles = tiles
        assert sum(w for _, w in tiles) == self.T
        self.parts = parts
        assert sum(n for _, n in parts) == self.NF
        self.Wmax = max(w for _, w in tiles)
        self.SW = self.KP * HP + nS * HS + nC * HC
        self.KW = max(self.KD, self.KM, max(n for _, n in parts))
        order = []
        i4 = self.KP + 3 * nS
        self.rounds = []
        R = max(nC, self.KP)
        for r in range(R):
            sj = r - SCONV_SHIFT
            self.rounds.append((r if r < nC else None, sj if 0 <= sj < nS else None, r if r < self.KP else None))
        self.sconv_tail = [j for j in range(nS) if j >= R - SCONV_SHIFT]
        for (ci, sj, pc) in self.rounds:
            if ci is not None:
                order += [i4 + ci, i4 + nC + ci]
            if sj is not None:
                order += [self.KP + sj, self.KP + nS + sj, self.KP + 2 * nS + sj]
            if pc is not None:
                order += [pc]
        for sj in self.sconv_tail:
            order += [self.KP + sj, self.KP + nS + sj, self.KP + 2 * nS + sj]
        self.in_order = order
        self.jj_of = {ch: jj for jj, ch in enumerate(order)}
        c = 0
        self.pl = []
        for l in range(L):
            d = {}
            for name, n in (("gmix", self.KD), ("gffn", self.KD), ("pscale", self.KP), ("sw", KS * nS),
                            ("cw", KC * nC), ("cb", nC), ("cg", nC), ("cbt", nC)):
                d[name] = c
                c += n
            self.pl.append(d)
        self.pg = {}
        for name, n in (("gfinal", self.KD), ("flag", 1), ("eps_rms", 1), ("eps_ln", 1), ("tbl", 4 * 16)):
            self.pg[name] = c
            c += n
        self.NPAR = c

    def st_off(self, kind, i):
        if kind == "p":
            return i * HP
        if kind == "s":
            return self.KP * HP + i * HS
        return self.KP * HP + self.nS * HS + i * HC

    def segs(self, ti):
        a, w = self.tiles[ti]
        b = a + w
        m_end = HALO + self.Lm
        out = []
        if a < m_end:
            out.append(("m", 0, min(b, m_end) - a))
        if b > m_end:
            assert a <= m_end
            out.append(("s", m_end - a, b - a))
        return out


FULL = Cfg(D=4096, PG=256, nS=12, nC=12, F=11008, L=2, Lm=2048, Ls=64,
           tiles=[(0, 448), (448, 448), (896, 448), (1344, 448), (1792, 384)],
           parts=[(0, 29), (29, 29), (58, 28)])


class Sched:
    def __init__(self):
        self.ops = {e: [] for e in ENGS}
        self.tick = {e: 0 for e in ENGS}
        self.dcount = {}
        self.seen = {e: {} for e in ENGS}
        self.lastw = {}
        self.readers = {}
        self.fence_dve = False

    def op(self, eng, fn, reads=(), writes=(), dsem=None, ndma=1, fence=False):
        if dsem is None:
            self.tick[eng] += 1
            done = (("e", eng), self.tick[eng])
        else:
            self.dcount[dsem] = self.dcount.get(dsem, 0) + 16 * ndma
            done = (("d", dsem), self.dcount[dsem])
        waits = {}

        def dep(d):
            k, v = d
            if k == ("e", eng):
                return
            if waits.get(k, 0) < v:
                waits[k] = v

        for r in reads:
            if r in self.lastw:
                dep(self.lastw[r])
        for w in writes:
            if w in self.lastw:
                dep(self.lastw[w])
            for k, v in self.readers.get(w, {}).items():
                dep((k, v))
        wl = []
        for k, v in waits.items():
            if self.seen[eng].get(k, 0) < v:
                self.seen[eng][k] = v
                wl.append((k, v))
        for r in reads:
            d = self.readers.setdefault(r, {})
            if d.get(done[0], 0) < done[1]:
                d[done[0]] = done[1]
        for w in writes:
            self.lastw[w] = done
            self.readers[w] = {}
        self.ops[eng].append((wl, fn, dsem, done[1] if (fence or ((FENCE_ALL or self.fence_dve) and eng == 'dve' and dsem is None)) else None))


def build_program(cfg):
    nc = bass.Bass("TRN2", target_bir_lowering=False)
    KD, KP, KM, NF, nS, nC, PGc, PG, L = cfg.KD, cfg.KP, cfg.KM, cfg.NF, cfg.nS, cfg.nC, cfg.PGc, cfg.PG, cfg.L
    Wmax, SW, KW = cfg.Wmax, cfg.SW, cfg.KW
    XW = sum(KD * w for _, w in cfg.tiles)
    XG = min(4, KD)

    def din(name, shape):
        return nc.dram_tensor(name, shape, F32, kind="ExternalInput").ap()

    def dout(name, shape):
        return nc.dram_tensor(name, shape, F32, kind="ExternalOutput").ap()

    d_xt = din("xt", [128, XW])
    d_win = din("win", [L * cfg.NJ * 128, KD * 128])
    d_wout = din("wout", [L * KD * 128, KM * 128])
    d_wgu = din("wgu", [L * NF * 2 * 128, KD * 128])
    d_wdn = [din("wdn%d" % q, [L * KD * 128, n * 128]) for q, (_, n) in enumerate(cfg.parts)]
    d_poolw = din("poolw", [128, L * 4 * PGc * PG])
    d_params = din("params", [128, cfg.NPAR])
    d_hsin = din("hsin", [128, L * SW])
    d_yo = dout("yo", [128, XW])
    d_ho = dout("ho", [128, L * SW])
    d_so = dout("so", [128, L * SW])

    wlist = []
    for ti in range(len(cfg.tiles)):
        for l in range(L):
            for jj in range(cfg.NJ):
                r0 = (l * cfg.NJ + jj) * 128
                wlist.append((d_win[r0:r0 + 128, :], KD * 128, ("in", l, jj)))
            for oc in range(KD):
                r0 = (l * KD + oc) * 128
                wlist.append((d_wout[r0:r0 + 128, :], KM * 128, ("out", l, oc)))
            for q, (f0, nf) in enumerate(cfg.parts):
                for fc in range(f0, f0 + nf):
                    for gu in range(2):
                        r0 = ((l * NF + fc) * 2 + gu) * 128
                        wlist.append((d_wgu[r0:r0 + 128, :], KD * 128, ("gu", l, fc, gu)))
                for oc in range(KD):
                    r0 = (l * KD + oc) * 128
                    wlist.append((d_wdn[q][r0:r0 + 128, :], nf * 128, ("dn", l, q, oc)))

    S = Sched()
    es = contextlib.ExitStack()
    with es:
        def sb(name, shape, dt):
            return es.enter_context(nc.sbuf_tensor("sb_" + name, shape, dt))

        x = sb("x", [128, KD, Wmax], F32)
        xn = sb("xn", [128, KD, Wmax], BF16)
        U = sb("U", [128, KM, Wmax], BF16)
        zc = sb("zc", [128, nC, Wmax], F32)
        wsl = [sb("w%d" % i, [128, KW, 128], BF16) for i in range(NSLOT)]
        SCW = max(w + len(cfg.segs(ti)) * HC for ti, (_, w) in enumerate(cfg.tiles))
        scr = [sb("scr%d" % i, [128, SCW], F32) for i in range(NSCR)]
        bfs = [sb("bf%d" % i, [128, Wmax], BF16) for i in range(NBF)]
        rs = sb("rs", [128, Wmax], F32)
        mu = sb("mu", [128, Wmax], F32)
        params = sb("params", [128, cfg.NPAR], F32)
        poolw = sb("poolw", [128, L * 4 * PGc, PG], BF16)
        hist = sb("hist", [128, L * SW], F32)
        hsamp = sb("hsamp", [128, L * SW], F32)
        onesD = sb("onesD", [128, 128], BF16)
        ps = [es.enter_context(nc.psum_tensor("ps%d" % i, [128, 512], F32)) for i in range(8)]
        esem = {e: es.enter_context(nc.semaphore("s_" + e)) for e in ENGS}
        dsem_names = ["w%d" % i for i in range(NSLOT)] + ["ld0", "ld1", "hst"] + ["xld%d" % i for i in range(XG)] + ["yst%d" % i for i in range(XG)]
        dsem = {n: es.enter_context(nc.semaphore("d_" + n)) for n in dsem_names}
        block = es.enter_context(nc.Block())

        def P(name, l=None):
            return cfg.pg[name] if l is None else cfg.pl[l][name]

        def pcol(col):
            return params[:, col:col + 1]

        hkeys_m = [("hist", l, cfg.st_off(k, i)) for l in range(L) for k, n in (("p", KP), ("s", nS), ("c", nC)) for i in range(n)]
        hkeys_s = [("hsamp", l, cfg.st_off(k, i)) for l in range(L) for k, n in (("p", KP), ("s", nS), ("c", nC)) for i in range(n)]
        def f_ld0(e):
            return [e.dma_start(out=params[:, :], in_=d_params[:, :]),
                    e.dma_start(out=hsamp[:, :], in_=d_hsin[:, :])]
        S.op("sp", f_ld0, writes=["params", "hsamp"] + hkeys_s, dsem="ld0", ndma=2)
        S.op("pool", lambda e: [e.dma_start(out=poolw[:].rearrange("p a b -> p (a b)"), in_=d_poolw[:, :])],
             writes=["poolw"], dsem="ld1")
        S.op("dve", lambda e: e.memset(hist[:, :], 0.0), writes=["hist"] + hkeys_m)
        S.op("dve", lambda e: e.memset(onesD[:, :], 1.0), writes=["onesD"])

        wstate = {"issued": 0, "used": 0}

        def issue_weight():
            n = wstate["issued"]
            if n >= len(wlist):
                return
            ap, ncols, _ = wlist[n]
            slot = n % NSLOT
            wstate["issued"] += 1
            S.op("pool", lambda e: [e.dma_start(out=wsl[slot][:].rearrange("p a b -> p (a b)")[:, :ncols], in_=ap)],
                 writes=[("w", slot)], dsem="w%d" % slot)

        def next_weight(tag):
            n = wstate["used"]
            assert wlist[n][2] == tag, (wlist[n][2], tag)
            wstate["used"] += 1
            return n % NSLOT

        for _ in range(NSLOT):
            issue_weight()

        ring = {"ps": 0, "scr": 0, "bf": 0}

        def nps():
            b = ring["ps"]
            ring["ps"] = (b + 1) % NRING
            return b

        def nscr():
            b = ring["scr"]
            ring["scr"] = (b + 1) % NSCR
            return b

        def nbf():
            b = ring["bf"]
            ring["bf"] = (b + 1) % NBF
            return b

        def mm_group(bank, W, pairs, reads, wslot=None):
            def fn(e):
                ins = None
                n = len(pairs)
                for i, (a, b) in enumerate(pairs):
                    ins = e.matmul(ps[bank][:, :W], a, b, start=(i == 0), stop=(i == n - 1))
                return ins
            rd = list(reads)
            if wslot is not None:
                rd.append(("w", wslot))
            S.op("pe", fn, reads=rd, writes=[("ps", bank)])
            if wslot is not None:
                issue_weight()

        pre = {}

        def prefetch_groups(W, specs):
            slots = [next_weight(tag) for (tag, nk) in specs]
            banks = [nps() for _ in specs]
            nk = specs[0][1]
            assert all(n == nk for _, n in specs)
            for k in range(nk):
                def fn(e, k=k):
                    ins = None
                    for slot, bank in zip(slots, banks):
                        ins = e.matmul(ps[bank][:, :W], wsl[slot][:, k, :], xn[:, k, :W], start=(k == 0), stop=(k == nk - 1))
                    return ins
                S.op("pe", fn, reads=[("xn", k)] + [("w", s_) for s_ in slots], writes=[("ps", b) for b in banks])
            for (tag, _), bank in zip(specs, banks):
                pre[tag] = bank
                issue_weight()

        def streamed_group(tag, W, nk, rhs_of, reads):
            if tag in pre:
                return pre.pop(tag)
            slot = next_weight(tag)
            bank = nps()
            pairs = [(wsl[slot][:, k, :], rhs_of(k)) for k in range(nk)]
            mm_group(bank, W, pairs, reads, wslot=slot)
            if auto_flush["on"]:
                flush_pending()
            return bank

        pending = []
        auto_flush = {"on": False}

        def flush_pending():
            while pending:
                pending.pop(0)()

        def rmsnorm(W, gcol, to_x, have_stats=False):
            flush_pending()
            for c in range(0 if not (have_stats and INC_STATS) else KD, KD):
                q = nbf()
                S.op("act", lambda e, c=c, q=q: e.activation(out=bfs[q][:, :W], in_=x[:, c, :W], func=AF.Square),
                     reads=[("x", c)], writes=[("bf", q)])
                S.op("pe", lambda e, c=c, q=q: e.matmul(ps[6][:, :W], onesD[:, :], bfs[q][:, :W],
                                                        start=(c == 0), stop=(c == KD - 1)),
                     reads=[("bf", q), "onesD"], writes=[("ps", 6)])
            s1 = nscr()
            S.op("act", lambda e: e.activation(out=scr[s1][:, :W], in_=ps[6][:, :W], func=AF.Sqrt,
                                               bias=pcol(P("eps_rms")), scale=1.0 / cfg.D),
                 reads=[("ps", 6), "params"], writes=[("scr", s1)])
            S.op("dve", lambda e: e.reciprocal(out=rs[:, :W], in_=scr[s1][:, :W]), reads=[("scr", s1)], writes=["rs"])
            for c in range(KD):
                if to_x:
                    S.op("dve", lambda e, c=c: e.scalar_tensor_tensor(out=x[:, c, :W], in0=x[:, c, :W],
                                                                      scalar=pcol(gcol + c), in1=rs[:, :W],
                                                                      op0=ALU.mult, op1=ALU.mult),
                         reads=[("x", c), "rs", "params"], writes=[("x", c)])
                else:
                    S.op("dve", lambda e, c=c: e.scalar_tensor_tensor(out=xn[:, c, :W], in0=x[:, c, :W],
                                                                      scalar=pcol(gcol + c), in1=rs[:, :W],
                                                                      op0=ALU.mult, op1=ALU.mult),
                         reads=[("x", c), "rs", "params"], writes=[("xn", c)])

        xn_all = [("xn", c) for c in range(KD)]

        def seg_layout(segs, H):
            out = []
            b = 0
            for (kind, c0, c1) in segs:
                out.append((kind, c0, c1, b))
                b += H + (c1 - c0)
            return out

        def state_ap(kind, l, off, H):
            t = hist if kind == "m" else hsamp
            return t[:, l * SW + off:l * SW + off + H], ("hist" if kind == "m" else "hsamp", l, off)

        def halo_mask(ti, pb, lay, H):
            if ti != 0:
                return
            kind, c0, c1, b = lay[0]
            S.op("dve", lambda e: e.tensor_scalar(out=scr[pb][:, b + H:b + H + HALO], in0=scr[pb][:, b + H:b + H + HALO],
                                                  scalar1=pcol(P("flag")), scalar2=None, op0=ALU.mult),
                 reads=[("scr", pb), "params"], writes=[("scr", pb)], fence=True)

        def hist_in(pb, lay, l, off, H):
            for (kind, c0, c1, b) in lay:
                sap, skey = state_ap(kind, l, off, H)
                S.op("dve", lambda e, sap=sap, b=b: e.tensor_copy(out=scr[pb][:, b:b + H], in_=sap),
                     reads=[skey], writes=[("scr", pb)], fence=True)

        def hist_out(pb, lay, l, off, H):
            for (kind, c0, c1, b) in lay:
                sap, skey = state_ap(kind, l, off, H)
                Ls_ = c1 - c0
                S.op("dve", lambda e, sap=sap, b=b, Ls_=Ls_: e.tensor_copy(out=sap, in_=scr[pb][:, b + Ls_:b + Ls_ + H]),
                     reads=[("scr", pb)], writes=[skey], fence=True)

        def cconv_head(ti, l, i, W, segs):
            lay = seg_layout(segs, HC)
            bA = streamed_group(("in", l, cfg.jj_of[KP + 3 * nS + i]), W, KD, lambda k: xn[:, k, :W], xn_all)
            bB = streamed_group(("in", l, cfg.jj_of[KP + 3 * nS + nC + i]), W, KD, lambda k: xn[:, k, :W], xn_all)
            flush_pending()
            st, sv, pb = nscr(), nscr(), nscr()
            S.op("act", lambda e: e.activation(out=scr[st][:, :W], in_=ps[bB][:, :W], func=AF.Tanh, scale=0.5),
                 reads=[("ps", bB)], writes=[("scr", st)])
            S.op("act", lambda e: e.activation(out=scr[sv][:, :W], in_=ps[bA][:, :W], func=AF.Copy, scale=0.5),
                 reads=[("ps", bA)], writes=[("scr", sv)])
            for (kind, c0, c1, b) in lay:
                S.op("dve", lambda e, c0=c0, c1=c1, b=b: e.scalar_tensor_tensor(
                    out=scr[pb][:, b + HC:b + HC + (c1 - c0)], in0=scr[st][:, c0:c1], scalar=1.0,
                    in1=scr[sv][:, c0:c1], op0=ALU.add, op1=ALU.mult),
                    reads=[("scr", st), ("scr", sv)], writes=[("scr", pb)])
            halo_mask(ti, pb, lay, HC)
            off = cfg.st_off("c", i)
            hist_in(pb, lay, l, off, HC)
            cw = P("cw", l) + i * KC
            for (kind, c0, c1, b) in lay:
                n = c1 - c0
                S.op("dve", lambda e, c0=c0, c1=c1, b=b, n=n: e.tensor_scalar(
                    out=zc[:, i, c0:c1], in0=scr[pb][:, b:b + n], scalar1=pcol(cw), scalar2=pcol(P("cb", l) + i),
                    op0=ALU.mult, op1=ALU.add),
                    reads=[("scr", pb), "params"], writes=[("zc", i)])
                for k in range(1, KC):
                    S.op("dve", lambda e, c0=c0, c1=c1, b=b, n=n, k=k: e.scalar_tensor_tensor(
                        out=zc[:, i, c0:c1], in0=scr[pb][:, b + k:b + k + n], scalar=pcol(cw + k),
                        in1=zc[:, i, c0:c1], op0=ALU.mult, op1=ALU.add),
                        reads=[("scr", pb), "params", ("zc", i)], writes=[("zc", i)])
            hist_out(pb, lay, l, off, HC)
            def stats_pe():
                q1, q2 = nbf(), nbf()
                S.op("act", lambda e: e.activation(out=bfs[q1][:, :W], in_=zc[:, i, :W], func=AF.Copy),
                     reads=[("zc", i)], writes=[("bf", q1)])
                S.op("act", lambda e: e.activation(out=bfs[q2][:, :W], in_=zc[:, i, :W], func=AF.Square),
                     reads=[("zc", i)], writes=[("bf", q2)])
                S.op("pe", lambda e: e.matmul(ps[6][:, :W], onesD[:, :], bfs[q1][:, :W], start=(i == 0), stop=(i == nC - 1)),
                     reads=[("bf", q1), "onesD"], writes=[("ps", 6)])
                S.op("pe", lambda e: e.matmul(ps[7][:, :W], onesD[:, :], bfs[q2][:, :W], start=(i == 0), stop=(i == nC - 1)),
                     reads=[("bf", q2), "onesD"], writes=[("ps", 7)])
            pending.append(stats_pe)

        def ln_finish(l, W):
            t = nscr()
            S.op("act", lambda e: e.activation(out=mu[:, :W], in_=ps[6][:, :W], func=AF.Copy, scale=1.0 / (128 * nC)),
                 reads=[("ps", 6)], writes=["mu"])
            S.op("dve", lambda e: e.scalar_tensor_tensor(out=scr[t][:, :W], in0=mu[:, :W], scalar=-1.0, in1=mu[:, :W],
                                                         op0=ALU.mult, op1=ALU.mult),
                 reads=["mu"], writes=[("scr", t)])
            S.op("dve", lambda e: e.scalar_tensor_tensor(out=scr[t][:, :W], in0=ps[7][:, :W], scalar=1.0 / (128 * nC), in1=scr[t][:, :W],
                                                         op0=ALU.mult, op1=ALU.add),
                 reads=[("ps", 7), ("scr", t)], writes=[("scr", t)])
            S.op("dve", lambda e: e.tensor_scalar(out=scr[t][:, :W], in0=scr[t][:, :W], scalar1=0.0, scalar2=None,
                                                  op0=ALU.max),
                 reads=[("scr", t)], writes=[("scr", t)])
            t2 = nscr()
            S.op("act", lambda e: e.activation(out=scr[t2][:, :W], in_=scr[t][:, :W], func=AF.Sqrt,
                                               bias=pcol(P("eps_ln")), scale=1.0),
                 reads=[("scr", t), "params"], writes=[("scr", t2)])
            S.op("dve", lambda e: e.reciprocal(out=rs[:, :W], in_=scr[t2][:, :W]), reads=[("scr", t2)], writes=["rs"])
            for i in range(nC):
                a = nscr()
                S.op("dve", lambda e, i=i, a=a: e.tensor_tensor(out=scr[a][:, :W], in0=zc[:, i, :W], in1=mu[:, :W],
                                                                op=ALU.subtract),
                     reads=[("zc", i), "mu"], writes=[("scr", a)])
                S.op("dve", lambda e, i=i, a=a: e.tensor_tensor(out=scr[a][:, :W], in0=scr[a][:, :W], in1=rs[:, :W],
                                                                op=ALU.mult),
                     reads=[("scr", a), "rs"], writes=[("scr", a)])
                S.op("act", lambda e, i=i, a=a: e.activation(out=U[:, KP + nS + i, :W], in_=scr[a][:, :W], func=AF.Silu,
                                                             bias=pcol(P("cbt", l) + i), scale=pcol(P("cg", l) + i)),
                     reads=[("scr", a), "params"], writes=[("U", KP + nS + i)])

        def pool_chunk(ti, l, c, W, segs, dq):
            lay = seg_layout(segs, HP)
            g = c // PGc
            kwin = POOL_WINDOWS[g]
            bA = streamed_group(("in", l, cfg.jj_of[c]), W, KD, lambda k: xn[:, k, :W], xn_all)
            pb = nscr()
            for (kind, c0, c1, b) in lay:
                S.op("act", lambda e, c0=c0, c1=c1, b=b: e.activation(out=scr[pb][:, b + HP:b + HP + (c1 - c0)],
                                                                     in_=ps[bA][:, c0:c1], func=AF.Copy),
                     reads=[("ps", bA)], writes=[("scr", pb)])
            halo_mask(ti, pb, lay, HP)
            off = cfg.st_off("p", c)
            hist_in(pb, lay, l, off, HP)
            src = pb
            for lev in range(g + 1):
                sh = 1 << lev
                lo = (1 << (lev + 1)) - 1
                dst = nscr()
                for (kind, c0, c1, b) in lay:
                    e0, e1 = b + lo, b + HP + (c1 - c0)
                    S.op("dve", lambda e, e0=e0, e1=e1, sh=sh, src=src, dst=dst: e.tensor_tensor(
                        out=scr[dst][:, e0:e1], in0=scr[src][:, e0:e1], in1=scr[src][:, e0 - sh:e1 - sh], op=ALU.add),
                        reads=[("scr", src)], writes=[("scr", dst)])
                src = dst
            for (kind, c0, c1, b) in lay:
                n = c1 - c0
                S.op("dve", lambda e, c0=c0, c1=c1, b=b, n=n, src=src: e.scalar_tensor_tensor(
                    out=bfs[dq][:, c0:c1], in0=scr[src][:, b + HP:b + HP + n], scalar=1.0 / kwin,
                    in1=scr[pb][:, b + HP:b + HP + n], op0=ALU.mult, op1=ALU.subtract),
                    reads=[("scr", src), ("scr", pb)], writes=[("bf", dq)])
            if ti == 0:
                b = lay[0][3]
                t = nscr()
                c0 = HALO
                S.op("dve", lambda e, src=src: e.tensor_tensor(out=scr[t][:, 0:16], in0=scr[src][:, b + HP + c0:b + HP + c0 + 16],
                                                               in1=params[:, P("tbl") + g * 16:P("tbl") + g * 16 + 16],
                                                               op=ALU.mult),
                     reads=[("scr", src), "params"], writes=[("scr", t)], fence=True)
                S.op("dve", lambda e: e.tensor_tensor(out=bfs[dq][:, c0:c0 + 16], in0=scr[t][:, 0:16],
                                                      in1=scr[pb][:, b + HP + c0:b + HP + c0 + 16], op=ALU.subtract),
                     reads=[("scr", t), ("scr", pb)], writes=[("bf", dq)], fence=True)
            hist_out(pb, lay, l, off, HP)

        def pool_group_mm(l, g, W, dqs):
            for oc in range(PGc):
                bank = nps()
                pairs = [(poolw[:, (l * 4 + g) * PGc + kc, oc * 128:(oc + 1) * 128], bfs[dqs[kc]][:, :W]) for kc in range(PGc)]
                mm_group(bank, W, pairs, [("bf", q) for q in dqs] + ["poolw"])
                ch = g * PGc + oc
                S.op("dve", lambda e, bank=bank, ch=ch: e.tensor_scalar(out=U[:, ch, :W], in0=ps[bank][:, :W],
                                                                        scalar1=pcol(P("pscale", l) + ch), scalar2=None,
                                                                        op0=ALU.mult),
                     reads=[("ps", bank), "params"], writes=[("U", ch)])

        def sconv_head(ti, l, i, W, segs):
            lay = seg_layout(segs, HS)
            bA = streamed_group(("in", l, cfg.jj_of[KP + i]), W, KD, lambda k: xn[:, k, :W], xn_all)
            bB = streamed_group(("in", l, cfg.jj_of[KP + nS + i]), W, KD, lambda k: xn[:, k, :W], xn_all)
            bC = streamed_group(("in", l, cfg.jj_of[KP + 2 * nS + i]), W, KD, lambda k: xn[:, k, :W], xn_all)
            sx, pb, sz = nscr(), nscr(), nscr()
            S.op("act", lambda e: e.activation(out=scr[sx][:, :W], in_=ps[bC][:, :W], func=AF.Copy),
                 reads=[("ps", bC)], writes=[("scr", sx)])
            for (kind, c0, c1, b) in lay:
                S.op("dve", lambda e, c0=c0, c1=c1, b=b: e.tensor_tensor(out=scr[pb][:, b + HS:b + HS + (c1 - c0)],
                                                                        in0=ps[bB][:, c0:c1], in1=scr[sx][:, c0:c1],
                                                                        op=ALU.mult),
                     reads=[("ps", bB), ("scr", sx)], writes=[("scr", pb)])
            halo_mask(ti, pb, lay, HS)
            off = cfg.st_off("s", i)
            hist_in(pb, lay, l, off, HS)
            sw = P("sw", l) + i * KS
            for (kind, c0, c1, b) in lay:
                n = c1 - c0
                S.op("dve", lambda e, c0=c0, c1=c1, b=b, n=n: e.tensor_scalar(out=scr[sz][:, c0:c1], in0=scr[pb][:, b:b + n],
                                                                             scalar1=pcol(sw), scalar2=None, op0=ALU.mult),
                     reads=[("scr", pb), "params"], writes=[("scr", sz)])
                for k in range(1, KS):
                    S.op("dve", lambda e, c0=c0, c1=c1, b=b, n=n, k=k: e.scalar_tensor_tensor(
                        out=scr[sz][:, c0:c1], in0=scr[pb][:, b + k:b + k + n], scalar=pcol(sw + k),
                        in1=scr[sz][:, c0:c1], op0=ALU.mult, op1=ALU.add),
                        reads=[("scr", pb), ("scr", sz), "params"], writes=[("scr", sz)])
            hist_out(pb, lay, l, off, HS)
            S.op("dve", lambda e: e.tensor_tensor(out=U[:, KP + i, :W], in0=ps[bA][:, :W], in1=scr[sz][:, :W], op=ALU.mult),
                 reads=[("ps", bA), ("scr", sz)], writes=[("U", KP + i)])

        def add_to_x(bank, oc, W, stats=False):
            S.op("dve", lambda e: e.tensor_tensor(out=x[:, oc, :W], in0=ps[bank][:, :W], in1=x[:, oc, :W], op=ALU.add),
                 reads=[("ps", bank), ("x", oc)], writes=[("x", oc)])
            if stats and INC_STATS:
                q = nbf()
                S.op("act", lambda e: e.activation(out=bfs[q][:, :W], in_=x[:, oc, :W], func=AF.Square),
                     reads=[("x", oc)], writes=[("bf", q)])
                pending.append(lambda: S.op("pe", lambda e: e.matmul(ps[6][:, :W], onesD[:, :], bfs[q][:, :W],
                                                                     start=(oc == 0), stop=(oc == KD - 1)),
                                            reads=[("bf", q), "onesD"], writes=[("ps", 6)]))

        xoff = 0
        for ti, (a0, W) in enumerate(cfg.tiles):
            segs = cfg.segs(ti)
            xsz = KD * W
            S.fence_dve = len(segs) > 1
            for gq in range(XG):
                c0, c1 = gq * KD // XG, (gq + 1) * KD // XG
                S.op("sp", lambda e, xoff=xoff, W=W, c0=c0, c1=c1: [e.dma_start(
                    out=x[:, c0:c1, :W], in_=d_xt[:, xoff + c0 * W:xoff + c1 * W].rearrange("p (c w) -> p c w", w=W))],
                    writes=[("x", c) for c in range(c0, c1)], dsem="xld%d" % gq)
            for l in range(L):
                rmsnorm(W, P("gmix", l), False, have_stats=(l > 0))
                if PREFETCH:
                    prefetch_groups(W, [(("in", l, jj), KD) for jj in range(min(NSLOT, cfg.NJ))])
                dqs = []
                for (ci, sj, pc) in cfg.rounds:
                    if ci is not None:
                        cconv_head(ti, l, ci, W, segs)
                    if sj is not None:
                        sconv_head(ti, l, sj, W, segs)
                    if pc is not None:
                        dq = nbf()
                        dqs.append(dq)
                        pool_chunk(ti, l, pc, W, segs, dq)
                        if len(dqs) == PGc:
                            pool_group_mm(l, pc // PGc, W, dqs)
                            dqs = []
                ln_done = False
                for sj in cfg.sconv_tail:
                    sconv_head(ti, l, sj, W, segs)
                    if not ln_done:
                        flush_pending()
                        ln_finish(l, W)
                        ln_done = True
                if not ln_done:
                    flush_pending()
                    ln_finish(l, W)
                U_all = [("U", c) for c in range(KM)]
                auto_flush["on"] = True
                for oc in range(KD):
                    bank = streamed_group(("out", l, oc), W, KM, lambda k: U[:, k, :W], U_all)
                    add_to_x(bank, oc, W, stats=True)
                auto_flush["on"] = False
                rmsnorm(W, P("gffn", l), False, have_stats=True)
                if PREFETCH:
                    f0_, nf0_ = cfg.parts[0]
                    specs = [(("gu", l, f0_ + j, gu), KD) for j in range(min(2, nf0_)) for gu in range(2)]
                    prefetch_groups(W, specs[:NSLOT])
                for q, (f0, nf) in enumerate(cfg.parts):
                    for j in range(nf):
                        fc = f0 + j
                        bG = streamed_group(("gu", l, fc, 0), W, KD, lambda k: xn[:, k, :W], xn_all)
                        bU = streamed_group(("gu", l, fc, 1), W, KD, lambda k: xn[:, k, :W], xn_all)
                        sg = nscr()
                        S.op("act", lambda e, bG=bG, sg=sg, W=W: e.activation(out=scr[sg][:, :W], in_=ps[bG][:, :W], func=AF.Silu),
                             reads=[("ps", bG)], writes=[("scr", sg)])
                        S.op("dve", lambda e, bU=bU, sg=sg, j=j, W=W: e.tensor_tensor(out=U[:, j, :W], in0=ps[bU][:, :W],
                                                                                 in1=scr[sg][:, :W], op=ALU.mult),
                             reads=[("ps", bU), ("scr", sg)], writes=[("U", j)])
                    h_all = [("U", j) for j in range(nf)]
                    last = (q == len(cfg.parts) - 1)
                    auto_flush["on"] = last
                    for oc in range(KD):
                        bank = streamed_group(("dn", l, q, oc), W, nf, lambda k: U[:, k, :W], h_all)
                        add_to_x(bank, oc, W, stats=last)
                    auto_flush["on"] = False
            rmsnorm(W, P("gfinal"), True, have_stats=True)
            for gq in range(XG):
                c0, c1 = gq * KD // XG, (gq + 1) * KD // XG
                S.op("sp", lambda e, xoff=xoff, W=W, c0=c0, c1=c1: [e.dma_start(
                    out=d_yo[:, xoff + c0 * W:xoff + c1 * W].rearrange("p (c w) -> p c w", w=W), in_=x[:, c0:c1, :W])],
                    reads=[("x", c) for c in range(c0, c1)], dsem="yst%d" % gq)
            xoff += xsz
        assert wstate["used"] == len(wlist)

        S.op("sp", lambda e: [e.dma_start(out=d_ho[:, :], in_=hist[:, :]), e.dma_start(out=d_so[:, :], in_=hsamp[:, :])],
             reads=hkeys_m + hkeys_s + ["hist", "hsamp"], dsem="hst", ndma=2)

        def semobj(k):
            return esem[k[1]] if k[0] == "e" else dsem[k[1]]

        def run(E, name):
            for (wl, fn, ds, fence) in S.ops[name]:
                for k, v in wl:
                    E.wait_ge(semobj(k), v)
                r = fn(E)
                if ds is None:
                    r.then_inc(esem[name], 1)
                    if fence is not None:
                        E.wait_ge(esem[name], fence)
                else:
                    for ins in r:
                        ins.then_inc(dsem[ds], 16)
            if name == "sp":
                for n in ["hst"] + ["yst%d" % i for i in range(XG)]:
                    E.wait_ge(dsem[n], S.dcount[n])

        @block.tensor
        def _(e):
            run(e, "pe")

        @block.scalar
        def _(e):
            run(e, "act")

        @block.vector
        def _(e):
            run(e, "dve")

        @block.gpsimd
        def _(e):
            run(e, "pool")

        @block.sync
        def _(e):
            run(e, "sp")
    return nc


def tile_w(w, kin, nout):
    return np.ascontiguousarray(w.reshape(kin, 128, nout, 128).transpose(2, 1, 0, 3)).reshape(nout, 128, kin * 128)


def chan_cols(v):
    return np.ascontiguousarray(v.reshape(-1, 128).T)


def prep_shared(cfg, inp):
    L, KD, KM, NF = cfg.L, cfg.KD, cfg.KM, cfg.NF
    win = np.stack([tile_w(inp["w_in"][l], KD, cfg.NJ)[cfg.in_order] for l in range(L)]).reshape(L * cfg.NJ * 128, KD * 128)
    wout = np.stack([tile_w(inp["w_out"][l], KM, KD) for l in range(L)]).reshape(L * KD * 128, KM * 128)
    wgu = np.stack([np.stack([tile_w(inp["w_gate"][l], KD, NF), tile_w(inp["w_up"][l], KD, NF)], axis=1) for l in range(L)])
    wgu = wgu.reshape(L * NF * 2 * 128, KD * 128)
    sh = {"win": win, "wout": wout, "wgu": wgu}
    for q, (f0, nf) in enumerate(cfg.parts):
        sh["wdn%d" % q] = np.stack([tile_w(inp["w_down"][l][f0 * 128:(f0 + nf) * 128], nf, KD) for l in range(L)]).reshape(L * KD * 128, nf * 128)
    pw = inp["pool_w"].reshape(L, 4, cfg.PGc, 128, cfg.PG).transpose(3, 0, 1, 2, 4)
    sh["poolw"] = np.ascontiguousarray(pw).reshape(128, L * 4 * cfg.PGc * cfg.PG)
    return sh


def prep_params(cfg, inp, half):
    p = np.zeros((128, cfg.NPAR), np.float32)
    nS, nC = cfg.nS, cfg.nC
    for l in range(cfg.L):
        d = cfg.pl[l]
        p[:, d["gmix"]:d["gmix"] + cfg.KD] = chan_cols(inp["norm_mix"][l])
        p[:, d["gffn"]:d["gffn"] + cfg.KD] = chan_cols(inp["norm_ffn"][l])
        p[:, d["pscale"]:d["pscale"] + cfg.KP] = chan_cols(inp["pool_scale"][l])
        sw = inp["sconv_w"][l].reshape(KS, nS, 128).transpose(2, 1, 0).reshape(128, nS * KS)
        p[:, d["sw"]:d["sw"] + nS * KS] = sw
        cw = inp["cconv_w"][l].reshape(KC, nC, 128).transpose(2, 1, 0).reshape(128, nC * KC)
        p[:, d["cw"]:d["cw"] + nC * KC] = cw
        p[:, d["cb"]:d["cb"] + nC] = chan_cols(inp["cconv_b"][l])
        p[:, d["cg"]:d["cg"] + nC] = chan_cols(inp["cnorm_g"][l])
        p[:, d["cbt"]:d["cbt"] + nC] = chan_cols(inp["cnorm_b"][l])
    g = cfg.pg
    p[:, g["gfinal"]:g["gfinal"] + cfg.KD] = chan_cols(inp["norm_final"])
    p[:, g["flag"]] = float(half)
    p[:, g["eps_rms"]] = RMS_EPS
    p[:, g["eps_ln"]] = LN_EPS
    for gi, k in enumerate(POOL_WINDOWS):
        for j in range(16):
            cnt = min(k, j + 1) if half == 0 else k
            p[:, g["tbl"] + gi * 16 + j] = np.float32(1.0) / np.float32(cnt)
    return p


def prep_hsin(cfg, inp, sb):
    h = np.zeros((128, cfg.L, cfg.SW), np.float32)
    for l in range(cfg.L):
        cp = inp["cache_pool"][l, sb]
        h[:, l, 0:cfg.KP * HP] = cp.reshape(HP, cfg.KP, 128).transpose(2, 1, 0).reshape(128, cfg.KP * HP)
        o = cfg.KP * HP
        cs = inp["cache_sconv"][l, sb]
        h[:, l, o:o + cfg.nS * HS] = cs.reshape(HS, cfg.nS, 128).transpose(2, 1, 0).reshape(128, cfg.nS * HS)
        o += cfg.nS * HS
        cc = inp["cache_cconv"][l, sb]
        h[:, l, o:o + cfg.nC * HC] = cc.reshape(HC, cfg.nC, 128).transpose(2, 1, 0).reshape(128, cfg.nC * HC)
    return h.reshape(128, cfg.L * cfg.SW)


def prep_x(cfg, xcols):
    out = []
    for (a, w) in cfg.tiles:
        blk = xcols[a:a + w].T.reshape(cfg.KD, 128, w).transpose(1, 0, 2).reshape(128, cfg.KD * w)
        out.append(blk)
    return np.ascontiguousarray(np.concatenate(out, axis=1))


def unprep_y(cfg, yo):
    rows = []
    off = 0
    for (a, w) in cfg.tiles:
        blk = yo[:, off:off + cfg.KD * w].reshape(128, cfg.KD, w).transpose(2, 1, 0).reshape(w, cfg.D)
        rows.append(blk)
        off += cfg.KD * w
    return np.concatenate(rows, axis=0)


def unprep_state(cfg, st):
    st = st.reshape(128, cfg.L, cfg.SW)
    pools, sconvs, cconvs = [], [], []
    for l in range(cfg.L):
        o = 0
        pools.append(st[:, l, o:o + cfg.KP * HP].reshape(128, cfg.KP, HP).transpose(2, 1, 0).reshape(HP, cfg.KP * 128))
        o += cfg.KP * HP
        sconvs.append(st[:, l, o:o + cfg.nS * HS].reshape(128, cfg.nS, HS).transpose(2, 1, 0).reshape(HS, cfg.nS * 128))
        o += cfg.nS * HS
        cconvs.append(st[:, l, o:o + cfg.nC * HC].reshape(128, cfg.nC, HC).transpose(2, 1, 0).reshape(HC, cfg.nC * 128))
    return np.stack(pools), np.stack(sconvs), np.stack(cconvs)


def run_cfg(cfg, inp, n_cores=8):
    inp = {k: np.asarray(v) for k, v in inp.items()}
    B = inp["x_prompt"].shape[0]
    assert 2 * B == n_cores and inp["x_sample"].shape[0] == n_cores
    Lm, D = cfg.Lm, cfg.D
    shared = prep_shared(cfg, inp)
    in_maps = []
    for core in range(n_cores):
        b, half = core // 2, core % 2
        xc = np.zeros((cfg.T, D), np.float32)
        if half == 1:
            xc[:HALO] = inp["x_prompt"][b, Lm - HALO:Lm]
        xc[HALO:HALO + Lm] = inp["x_prompt"][b, half * Lm:(half + 1) * Lm]
        xc[HALO + Lm:] = inp["x_sample"][core]
        m = dict(shared)
        m["xt"] = prep_x(cfg, xc)
        m["params"] = prep_params(cfg, inp, half)
        m["hsin"] = prep_hsin(cfg, inp, core)
        in_maps.append(m)
    nc = build_program(cfg)
    res = run_bass_kernel_spmd(nc, in_maps, core_ids=list(range(n_cores)))
    S_, L = 2 * Lm, cfg.L
    y_prompt = np.zeros((B, S_, D), np.float32)
    y_sample = np.zeros((n_cores, cfg.Ls, D), np.float32)
    DP, DS, DC = cfg.KP * 128, cfg.nS * 128, cfg.nC * 128
    pool_p = np.zeros((L, B, HP, DP), np.float32)
    pool_s = np.zeros((L, n_cores, HP, DP), np.float32)
    sc_p = np.zeros((L, B, HS, DS), np.float32)
    sc_s = np.zeros((L, n_cores, HS, DS), np.float32)
    cc_p = np.zeros((L, B, HC, DC), np.float32)
    cc_s = np.zeros((L, n_cores, HC, DC), np.float32)
    for core in range(n_cores):
        b, half = core // 2, core % 2
        r = res.results[core]
        Y = unprep_y(cfg, np.asarray(r["yo"]))
        y_prompt[b, half * Lm:(half + 1) * Lm] = Y[HALO:HALO + Lm]
        y_sample[core] = Y[HALO + Lm:]
        p, s, c = unprep_state(cfg, np.asarray(r["so"]))
        pool_s[:, core], sc_s[:, core], cc_s[:, core] = p, s, c
        if half == 1:
            p, s, c = unprep_state(cfg, np.asarray(r["ho"]))
            pool_p[:, b], sc_p[:, b], cc_p[:, b] = p, s, c
    return (y_prompt, y_sample, pool_p, pool_s, sc_p, sc_s, cc_p, cc_s)


def kernel(**inputs):
    return run_cfg(FULL, inputs)
```

```python
import os
import contextlib
import numpy as np
import concourse.bass as bass
import concourse.mybir as mybir
from concourse.bass_utils import run_bass_kernel_spmd

F32 = mybir.dt.float32
BF16 = mybir.dt.bfloat16
AF = mybir.ActivationFunctionType
ALU = mybir.AluOpType

ENGS = ["pe", "act", "dve", "pool", "sp"]
POOL_WINDOWS = (2, 4, 8, 16)
HP, HS, HC = 15, 2, 30
KS, KC = 3, 31
RMS_EPS = 1e-6
LN_EPS = 1e-5
HALO = 64
NSLOT = int(os.environ.get('K_NSLOT', '5'))
NSUPER = 4
NRING = 6
NSCR = int(os.environ.get('K_NSCR', '6'))
NBF = int(os.environ.get('K_NBF', '4'))
import os
SCONV_SHIFT = int(os.environ.get('K_SHIFT', '2'))
FENCE_ALL = os.environ.get('K_FENCEALL', '0') == '1'
PREFETCH = os.environ.get('K_PREFETCH', '1') == '1'
INC_STATS = os.environ.get('K_STATS', '1') == '1'


class Cfg:
    def __init__(self, D, PG, nS, nC, F, L, Lm, Ls, tiles, parts):
        self.D, self.PG, self.nS, self.nC, self.F, self.L = D, PG, nS, nC, F, L
        self.KD = D // 128
        self.PGc = PG // 128
        self.KP = 4 * self.PGc
        self.KM = self.KP + nS + nC
        self.NF = F // 128
        self.NJ = self.KP + 3 * nS + 2 * nC
        self.Lm, self.Ls = Lm, Ls
        self.T = HALO + Lm + Ls
        self.tiles = tiles
        assert sum(w for _, w in tiles) == self.T
        self.parts = parts
        assert sum(n for _, n in parts) == self.NF
        self.Wmax = max(w for _, w in tiles)
        self.SW = self.KP * HP + nS * HS + nC * HC
        self.KW = max(self.KD, self.KM, max(n for _, n in parts))
        order = []
        i4 = self.KP + 3 * nS
        self.rounds = []
        R = max(nC, self.KP)
        for r in range(R):
            sj = r - SCONV_SHIFT
            self.rounds.append((r if r < nC else None, sj if 0 <= sj < nS else None, r if r < self.KP else None))
        self.sconv_tail = [j for j in range(nS) if j >= R - SCONV_SHIFT]
        for (ci, sj, pc) in self.rounds:
            if ci is not None:
                order += [i4 + ci, i4 + nC + ci]
            if sj is not None:
                order += [self.KP + sj, self.KP + nS + sj, self.KP + 2 * nS + sj]
            if pc is not None:
                order += [pc]
        for sj in self.sconv_tail:
            order += [self.KP + sj, self.KP + nS + sj, self.KP + 2 * nS + sj]
        self.in_order = order
        self.jj_of = {ch: jj for jj, ch in enumerate(order)}
        c = 0
        self.pl = []
        for l in range(L):
            d = {}
            for name, n in (("gmix", self.KD), ("gffn", self.KD), ("pscale", self.KP), ("sw", KS * nS),
                            ("cw", KC * nC), ("cb", nC), ("cg", nC), ("cbt", nC)):
                d[name] = c
                c += n
            self.pl.append(d)
        self.pg = {}
        for name, n in (("gfinal", self.KD), ("flag", 1), ("eps_rms", 1), ("eps_ln", 1), ("tbl", 4 * 16)):
            self.pg[name] = c
            c += n
        self.NPAR = c

    def st_off(self, kind, i):
        if kind == "p":
            return i * HP
        if kind == "s":
            return self.KP * HP + i * HS
        return self.KP * HP + self.nS * HS + i * HC

    def segs(self, ti):
        a, w = self.tiles[ti]
        b = a + w
        m_end = HALO + self.Lm
        out = []
        if a < m_end:
            out.append(("m", 0, min(b, m_end) - a))
        if b > m_end:
            assert a <= m_end
            out.append(("s", m_end - a, b - a))
        return out


FULL = Cfg(D=4096, PG=256, nS=12, nC=12, F=11008, L=2, Lm=2048, Ls=64,
           tiles=[(0, 448), (448, 448), (896, 448), (1344, 448), (1792, 384)],
           parts=[(0, 29), (29, 29), (58, 28)])


class Sched:
    def __init__(self):
        self.ops = {e: [] for e in ENGS}
        self.tick = {e: 0 for e in ENGS}
        self.dcount = {}
        self.seen = {e: {} for e in ENGS}
        self.lastw = {}
        self.readers = {}
        self.fence_dve = False

    def op(self, eng, fn, reads=(), writes=(), dsem=None, ndma=1, fence=False):
        if dsem is None:
            self.tick[eng] += 1
            done = (("e", eng), self.tick[eng])
        else:
            self.dcount[dsem] = self.dcount.get(dsem, 0) + 16 * ndma
            done = (("d", dsem), self.dcount[dsem])
        waits = {}

        def dep(d):
            k, v = d
            if k == ("e", eng):
                return
            if waits.get(k, 0) < v:
                waits[k] = v

        for r in reads:
            if r in self.lastw:
                dep(self.lastw[r])
        for w in writes:
            if w in self.lastw:
                dep(self.lastw[w])
            for k, v in self.readers.get(w, {}).items():
                dep((k, v))
        wl = []
        for k, v in waits.items():
            if self.seen[eng].get(k, 0) < v:
                self.seen[eng][k] = v
                wl.append((k, v))
        for r in reads:
            d = self.readers.setdefault(r, {})
            if d.get(done[0], 0) < done[1]:
                d[done[0]] = done[1]
        for w in writes:
            self.lastw[w] = done
            self.readers[w] = {}
        self.ops[eng].append((wl, fn, dsem, done[1] if (fence or ((FENCE_ALL or self.fence_dve) and eng == 'dve' and dsem is None)) else None))


def build_program(cfg):
    nc = bass.Bass("TRN2", target_bir_lowering=False)
    KD, KP, KM, NF, nS, nC, PGc, PG, L = cfg.KD, cfg.KP, cfg.KM, cfg.NF, cfg.nS, cfg.nC, cfg.PGc, cfg.PG, cfg.L
    Wmax, SW, KW = cfg.Wmax, cfg.SW, cfg.KW
    XW = sum(KD * w for _, w in cfg.tiles)
    XG = min(4, KD)

    def din(name, shape):
        return nc.dram_tensor(name, shape, F32, kind="ExternalInput").ap()

    def dout(name, shape):
        return nc.dram_tensor(name, shape, F32, kind="ExternalOutput").ap()

    d_xt = din("xt", [128, XW])
    d_win = din("win", [L * cfg.NJ * 128, KD * 128])
    d_wout = din("wout", [L * KD * 128, KM * 128])
    d_wgu = din("wgu", [L * NF * 2 * 128, KD * 128])
    d_wdn = [din("wdn%d" % q, [L * KD * 128, n * 128]) for q, (_, n) in enumerate(cfg.parts)]
    d_poolw = din("poolw", [128, L * 4 * PGc * PG])
    d_params = din("params", [128, cfg.NPAR])
    d_hsin = din("hsin", [128, L * SW])
    d_yo = dout("yo", [128, XW])
    d_ho = dout("ho", [128, L * SW])
    d_so = dout("so", [128, L * SW])

    wlist = []
    for ti in range(len(cfg.tiles)):
        for l in range(L):
            for jj in range(cfg.NJ):
                r0 = (l * cfg.NJ + jj) * 128
                wlist.append((d_win[r0:r0 + 128, :], KD * 128, ("in", l, jj)))
            for oc in range(KD):
                r0 = (l * KD + oc) * 128
                wlist.append((d_wout[r0:r0 + 128, :], KM * 128, ("out", l, oc)))
            for q, (f0, nf) in enumerate(cfg.parts):
                for fc in range(f0, f0 + nf):
                    for gu in range(2):
                        r0 = ((l * NF + fc) * 2 + gu) * 128
                        wlist.append((d_wgu[r0:r0 + 128, :], KD * 128, ("gu", l, fc, gu)))
                for oc in range(KD):
                    r0 = (l * KD + oc) * 128
                    wlist.append((d_wdn[q][r0:r0 + 128, :], nf * 128, ("dn", l, q, oc)))

    S = Sched()
    es = contextlib.ExitStack()
    with es:
        def sb(name, shape, dt):
            return es.enter_context(nc.sbuf_tensor("sb_" + name, shape, dt))

        x = sb("x", [128, KD, Wmax], F32)
        xn = sb("xn", [128, KD, Wmax], BF16)
        U = sb("U", [128, KM, Wmax], BF16)
        zc = sb("zc", [128, nC, Wmax], F32)
        wsl = [sb("w%d" % i, [128, KW, 128], BF16) for i in range(NSLOT)]
        SCW = max(w + len(cfg.segs(ti)) * HC for ti, (_, w) in enumerate(cfg.tiles))
        scr = [sb("scr%d" % i, [128, SCW], F32) for i in range(NSCR)]
        bfs = [sb("bf%d" % i, [128, Wmax], BF16) for i in range(NBF)]
        rs = sb("rs", [128, Wmax], F32)
        mu = sb("mu", [128, Wmax], F32)
        params = sb("params", [128, cfg.NPAR], F32)
        poolw = sb("poolw", [128, 4 * PGc, PG], BF16)
        hist = sb("hist", [128, L * SW], F32)
        hsamp = sb("hsamp", [128, L * SW], F32)
        onesD = sb("onesD", [128, 128], BF16)
        ps = [es.enter_context(nc.psum_tensor("ps%d" % i, [128, 512], F32)) for i in range(8)]
        esem = {e: es.enter_context(nc.semaphore("s_" + e)) for e in ENGS}
        dsem_names = ["w%d" % i for i in range(NSLOT)] + ["ld0", "ld1", "hst"] + ["xld%d" % i for i in range(XG)] + ["yst%d" % i for i in range(XG)]
        dsem = {n: es.enter_context(nc.semaphore("d_" + n)) for n in dsem_names}
        block = es.enter_context(nc.Block())

        def P(name, l=None):
            return cfg.pg[name] if l is None else cfg.pl[l][name]

        def pcol(col):
            return params[:, col:col + 1]

        hkeys_m = [("hist", l, cfg.st_off(k, i)) for l in range(L) for k, n in (("p", KP), ("s", nS), ("c", nC)) for i in range(n)]
        hkeys_s = [("hsamp", l, cfg.st_off(k, i)) for l in range(L) for k, n in (("p", KP), ("s", nS), ("c", nC)) for i in range(n)]
        def f_ld0(e):
            return [e.dma_start(out=params[:, :], in_=d_params[:, :]),
                    e.dma_start(out=hsamp[:, :], in_=d_hsin[:, :])]
        S.op("sp", f_ld0, writes=["params", "hsamp"] + hkeys_s, dsem="ld0", ndma=2)
        PWL = 4 * PGc * PG

        def load_poolw(l):
            S.op("pool", lambda e: [e.dma_start(out=poolw[:].rearrange("p a b -> p (a b)"), in_=d_poolw[:, l * PWL:(l + 1) * PWL])],
                 writes=["poolw"], dsem="ld1")
        load_poolw(0)
        S.op("dve", lambda e: e.memset(hist[:, :], 0.0), writes=["hist"] + hkeys_m)
        S.op("dve", lambda e: e.memset(onesD[:, :], 1.0), writes=["onesD"])

        wstate = {"issued": 0, "used": 0}

        def issue_weight():
            n = wstate["issued"]
            if n >= len(wlist):
                return
            ap, ncols, _ = wlist[n]
            slot = n % NSLOT
            wstate["issued"] += 1
            S.op("pool", lambda e: [e.dma_start(out=wsl[slot][:].rearrange("p a b -> p (a b)")[:, :ncols], in_=ap)],
                 writes=[("w", slot)], dsem="w%d" % slot)

        def next_weight(tag):
            n = wstate["used"]
            assert wlist[n][2] == tag, (wlist[n][2], tag)
            wstate["used"] += 1
            return n % NSLOT

        for _ in range(NSLOT):
            issue_weight()

        ring = {"ps": 0, "scr": 0, "bf": 0}

        def nps():
            b = ring["ps"]
            ring["ps"] = (b + 1) % NRING
            return b

        def nscr():
            b = ring["scr"]
            ring["scr"] = (b + 1) % NSCR
            return b

        def nbf():
            b = ring["bf"]
            ring["bf"] = (b + 1) % NBF
            return b

        def mm_group(bank, W, pairs, reads, wslot=None):
            def fn(e):
                ins = None
                n = len(pairs)
                for i, (a, b) in enumerate(pairs):
                    ins = e.matmul(ps[bank][:, :W], a, b, start=(i == 0), stop=(i == n - 1))
                return ins
            rd = list(reads)
            if wslot is not None:
                rd.append(("w", wslot))
            S.op("pe", fn, reads=rd, writes=[("ps", bank)])
            if wslot is not None:
                issue_weight()

        pre = {}

        def prefetch_groups(W, specs):
            slots = [next_weight(tag) for (tag, nk) in specs]
            banks = [nps() for _ in specs]
            nk = specs[0][1]
            assert all(n == nk for _, n in specs)
            for k in range(nk):
                def fn(e, k=k):
                    ins = None
                    for slot, bank in zip(slots, banks):
                        ins = e.matmul(ps[bank][:, :W], wsl[slot][:, k, :], xn[:, k, :W], start=(k == 0), stop=(k == nk - 1))
                    return ins
                S.op("pe", fn, reads=[("xn", k)] + [("w", s_) for s_ in slots], writes=[("ps", b) for b in banks])
            for (tag, _), bank in zip(specs, banks):
                pre[tag] = bank
                issue_weight()

        def streamed_group(tag, W, nk, rhs_of, reads):
            if tag in pre:
                return pre.pop(tag)
            slot = next_weight(tag)
            bank = nps()
            pairs = [(wsl[slot][:, k, :], rhs_of(k)) for k in range(nk)]
            mm_group(bank, W, pairs, reads, wslot=slot)
            if auto_flush["on"]:
                flush_pending()
            return bank

        pending = []
        auto_flush = {"on": False}

        def flush_pending():
            while pending:
                pending.pop(0)()

        def rmsnorm(W, gcol, to_x, have_stats=False):
            flush_pending()
            for c in range(0 if not (have_stats and INC_STATS) else KD, KD):
                q = nbf()
                S.op("act", lambda e, c=c, q=q: e.activation(out=bfs[q][:, :W], in_=x[:, c, :W], func=AF.Square),
                     reads=[("x", c)], writes=[("bf", q)])
                S.op("pe", lambda e, c=c, q=q: e.matmul(ps[6][:, :W], onesD[:, :], bfs[q][:, :W],
                                                        start=(c == 0), stop=(c == KD - 1)),
                     reads=[("bf", q), "onesD"], writes=[("ps", 6)])
            s1 = nscr()
            S.op("act", lambda e: e.activation(out=scr[s1][:, :W], in_=ps[6][:, :W], func=AF.Sqrt,
                                               bias=pcol(P("eps_rms")), scale=1.0 / cfg.D),
                 reads=[("ps", 6), "params"], writes=[("scr", s1)])
            S.op("dve", lambda e: e.reciprocal(out=rs[:, :W], in_=scr[s1][:, :W]), reads=[("scr", s1)], writes=["rs"])
            for c in range(KD):
                if to_x:
                    S.op("dve", lambda e, c=c: e.scalar_tensor_tensor(out=x[:, c, :W], in0=x[:, c, :W],
                                                                      scalar=pcol(gcol + c), in1=rs[:, :W],
                                                                      op0=ALU.mult, op1=ALU.mult),
                         reads=[("x", c), "rs", "params"], writes=[("x", c)])
                else:
                    S.op("dve", lambda e, c=c: e.scalar_tensor_tensor(out=xn[:, c, :W], in0=x[:, c, :W],
                                                                      scalar=pcol(gcol + c), in1=rs[:, :W],
                                                                      op0=ALU.mult, op1=ALU.mult),
                         reads=[("x", c), "rs", "params"], writes=[("xn", c)])

        xn_all = [("xn", c) for c in range(KD)]

        def seg_layout(segs, H):
            out = []
            b = 0
            for (kind, c0, c1) in segs:
                out.append((kind, c0, c1, b))
                b += H + (c1 - c0)
            return out

        def state_ap(kind, l, off, H):
            t = hist if kind == "m" else hsamp
            return t[:, l * SW + off:l * SW + off + H], ("hist" if kind == "m" else "hsamp", l, off)

        def halo_mask(ti, pb, lay, H):
            if ti != 0:
                return
            kind, c0, c1, b = lay[0]
            S.op("dve", lambda e: e.tensor_scalar(out=scr[pb][:, b + H:b + H + HALO], in0=scr[pb][:, b + H:b + H + HALO],
                                                  scalar1=pcol(P("flag")), scalar2=None, op0=ALU.mult),
                 reads=[("scr", pb), "params"], writes=[("scr", pb)], fence=True)

        def hist_in(pb, lay, l, off, H):
            for (kind, c0, c1, b) in lay:
                sap, skey = state_ap(kind, l, off, H)
                S.op("dve", lambda e, sap=sap, b=b: e.tensor_copy(out=scr[pb][:, b:b + H], in_=sap),
                     reads=[skey], writes=[("scr", pb)], fence=True)

        def hist_out(pb, lay, l, off, H):
            for (kind, c0, c1, b) in lay:
                sap, skey = state_ap(kind, l, off, H)
                Ls_ = c1 - c0
                S.op("dve", lambda e, sap=sap, b=b, Ls_=Ls_: e.tensor_copy(out=sap, in_=scr[pb][:, b + Ls_:b + Ls_ + H]),
                     reads=[("scr", pb)], writes=[skey], fence=True)

        def cconv_head(ti, l, i, W, segs):
            lay = seg_layout(segs, HC)
            bA = streamed_group(("in", l, cfg.jj_of[KP + 3 * nS + i]), W, KD, lambda k: xn[:, k, :W], xn_all)
            bB = streamed_group(("in", l, cfg.jj_of[KP + 3 * nS + nC + i]), W, KD, lambda k: xn[:, k, :W], xn_all)
            flush_pending()
            st, sv, pb = nscr(), nscr(), nscr()
            S.op("act", lambda e: e.activation(out=scr[st][:, :W], in_=ps[bB][:, :W], func=AF.Tanh, scale=0.5),
                 reads=[("ps", bB)], writes=[("scr", st)])
            S.op("act", lambda e: e.activation(out=scr[sv][:, :W], in_=ps[bA][:, :W], func=AF.Copy, scale=0.5),
                 reads=[("ps", bA)], writes=[("scr", sv)])
            for (kind, c0, c1, b) in lay:
                S.op("dve", lambda e, c0=c0, c1=c1, b=b: e.scalar_tensor_tensor(
                    out=scr[pb][:, b + HC:b + HC + (c1 - c0)], in0=scr[st][:, c0:c1], scalar=1.0,
                    in1=scr[sv][:, c0:c1], op0=ALU.add, op1=ALU.mult),
                    reads=[("scr", st), ("scr", sv)], writes=[("scr", pb)])
            halo_mask(ti, pb, lay, HC)
            off = cfg.st_off("c", i)
            hist_in(pb, lay, l, off, HC)
            cw = P("cw", l) + i * KC
            for (kind, c0, c1, b) in lay:
                n = c1 - c0
                S.op("dve", lambda e, c0=c0, c1=c1, b=b, n=n: e.tensor_scalar(
                    out=zc[:, i, c0:c1], in0=scr[pb][:, b:b + n], scalar1=pcol(cw), scalar2=pcol(P("cb", l) + i),
                    op0=ALU.mult, op1=ALU.add),
                    reads=[("scr", pb), "params"], writes=[("zc", i)])
                for k in range(1, KC):
                    S.op("dve", lambda e, c0=c0, c1=c1, b=b, n=n, k=k: e.scalar_tensor_tensor(
                        out=zc[:, i, c0:c1], in0=scr[pb][:, b + k:b + k + n], scalar=pcol(cw + k),
                        in1=zc[:, i, c0:c1], op0=ALU.mult, op1=ALU.add),
                        reads=[("scr", pb), "params", ("zc", i)], writes=[("zc", i)])
            hist_out(pb, lay, l, off, HC)
            def stats_pe():
                q1, q2 = nbf(), nbf()
                S.op("act", lambda e: e.activation(out=bfs[q1][:, :W], in_=zc[:, i, :W], func=AF.Copy),
                     reads=[("zc", i)], writes=[("bf", q1)])
                S.op("act", lambda e: e.activation(out=bfs[q2][:, :W], in_=zc[:, i, :W], func=AF.Square),
                     reads=[("zc", i)], writes=[("bf", q2)])
                S.op("pe", lambda e: e.matmul(ps[6][:, :W], onesD[:, :], bfs[q1][:, :W], start=(i == 0), stop=(i == nC - 1)),
                     reads=[("bf", q1), "onesD"], writes=[("ps", 6)])
                S.op("pe", lambda e: e.matmul(ps[7][:, :W], onesD[:, :], bfs[q2][:, :W], start=(i == 0), stop=(i == nC - 1)),
                     reads=[("bf", q2), "onesD"], writes=[("ps", 7)])
            pending.append(stats_pe)

        def ln_finish(l, W):
            t = nscr()
            S.op("act", lambda e: e.activation(out=mu[:, :W], in_=ps[6][:, :W], func=AF.Copy, scale=1.0 / (128 * nC)),
                 reads=[("ps", 6)], writes=["mu"])
            S.op("dve", lambda e: e.scalar_tensor_tensor(out=scr[t][:, :W], in0=mu[:, :W], scalar=-1.0, in1=mu[:, :W],
                                                         op0=ALU.mult, op1=ALU.mult),
                 reads=["mu"], writes=[("scr", t)])
            S.op("dve", lambda e: e.scalar_tensor_tensor(out=scr[t][:, :W], in0=ps[7][:, :W], scalar=1.0 / (128 * nC), in1=scr[t][:, :W],
                                                         op0=ALU.mult, op1=ALU.add),
                 reads=[("ps", 7), ("scr", t)], writes=[("scr", t)])
            S.op("dve", lambda e: e.tensor_scalar(out=scr[t][:, :W], in0=scr[t][:, :W], scalar1=0.0, scalar2=None,
                                                  op0=ALU.max),
                 reads=[("scr", t)], writes=[("scr", t)])
            t2 = nscr()
            S.op("act", lambda e: e.activation(out=scr[t2][:, :W], in_=scr[t][:, :W], func=AF.Sqrt,
                                               bias=pcol(P("eps_ln")), scale=1.0),
                 reads=[("scr", t), "params"], writes=[("scr", t2)])
            S.op("dve", lambda e: e.reciprocal(out=rs[:, :W], in_=scr[t2][:, :W]), reads=[("scr", t2)], writes=["rs"])
            for i in range(nC):
                a = nscr()
                S.op("dve", lambda e, i=i, a=a: e.tensor_tensor(out=scr[a][:, :W], in0=zc[:, i, :W], in1=mu[:, :W],
                                                                op=ALU.subtract),
                     reads=[("zc", i), "mu"], writes=[("scr", a)])
                S.op("dve", lambda e, i=i, a=a: e.tensor_tensor(out=scr[a][:, :W], in0=scr[a][:, :W], in1=rs[:, :W],
                                                                op=ALU.mult),
                     reads=[("scr", a), "rs"], writes=[("scr", a)])
                S.op("act", lambda e, i=i, a=a: e.activation(out=U[:, KP + nS + i, :W], in_=scr[a][:, :W], func=AF.Silu,
                                                             bias=pcol(P("cbt", l) + i), scale=pcol(P("cg", l) + i)),
                     reads=[("scr", a), "params"], writes=[("U", KP + nS + i)])

        def pool_chunk(ti, l, c, W, segs, dq):
            lay = seg_layout(segs, HP)
            g = c // PGc
            kwin = POOL_WINDOWS[g]
            bA = streamed_group(("in", l, cfg.jj_of[c]), W, KD, lambda k: xn[:, k, :W], xn_all)
            pb = nscr()
            for (kind, c0, c1, b) in lay:
                S.op("act", lambda e, c0=c0, c1=c1, b=b: e.activation(out=scr[pb][:, b + HP:b + HP + (c1 - c0)],
                                                                     in_=ps[bA][:, c0:c1], func=AF.Copy),
                     reads=[("ps", bA)], writes=[("scr", pb)])
            halo_mask(ti, pb, lay, HP)
            off = cfg.st_off("p", c)
            hist_in(pb, lay, l, off, HP)
            src = pb
            for lev in range(g + 1):
                sh = 1 << lev
                lo = (1 << (lev + 1)) - 1
                dst = nscr()
                for (kind, c0, c1, b) in lay:
                    e0, e1 = b + lo, b + HP + (c1 - c0)
                    S.op("dve", lambda e, e0=e0, e1=e1, sh=sh, src=src, dst=dst: e.tensor_tensor(
                        out=scr[dst][:, e0:e1], in0=scr[src][:, e0:e1], in1=scr[src][:, e0 - sh:e1 - sh], op=ALU.add),
                        reads=[("scr", src)], writes=[("scr", dst)])
                src = dst
            for (kind, c0, c1, b) in lay:
                n = c1 - c0
                S.op("dve", lambda e, c0=c0, c1=c1, b=b, n=n, src=src: e.scalar_tensor_tensor(
                    out=bfs[dq][:, c0:c1], in0=scr[src][:, b + HP:b + HP + n], scalar=1.0 / kwin,
                    in1=scr[pb][:, b + HP:b + HP + n], op0=ALU.mult, op1=ALU.subtract),
                    reads=[("scr", src), ("scr", pb)], writes=[("bf", dq)])
            if ti == 0:
                b = lay[0][3]
                t = nscr()
                c0 = HALO
                S.op("dve", lambda e, src=src: e.tensor_tensor(out=scr[t][:, 0:16], in0=scr[src][:, b + HP + c0:b + HP + c0 + 16],
                                                               in1=params[:, P("tbl") + g * 16:P("tbl") + g * 16 + 16],
                                                               op=ALU.mult),
                     reads=[("scr", src), "params"], writes=[("scr", t)], fence=True)
                S.op("dve", lambda e: e.tensor_tensor(out=bfs[dq][:, c0:c0 + 16], in0=scr[t][:, 0:16],
                                                      in1=scr[pb][:, b + HP + c0:b + HP + c0 + 16], op=ALU.subtract),
                     reads=[("scr", t), ("scr", pb)], writes=[("bf", dq)], fence=True)
            hist_out(pb, lay, l, off, HP)

        def pool_group_mm(l, g, W, dqs):
            for oc in range(PGc):
                bank = nps()
                pairs = [(poolw[:, g * PGc + kc, oc * 128:(oc + 1) * 128], bfs[dqs[kc]][:, :W]) for kc in range(PGc)]
                mm_group(bank, W, pairs, [("bf", q) for q in dqs] + ["poolw"])
                ch = g * PGc + oc
                S.op("dve", lambda e, bank=bank, ch=ch: e.tensor_scalar(out=U[:, ch, :W], in0=ps[bank][:, :W],
                                                                        scalar1=pcol(P("pscale", l) + ch), scalar2=None,
                                                                        op0=ALU.mult),
                     reads=[("ps", bank), "params"], writes=[("U", ch)])

        def sconv_head(ti, l, i, W, segs):
            lay = seg_layout(segs, HS)
            bA = streamed_group(("in", l, cfg.jj_of[KP + i]), W, KD, lambda k: xn[:, k, :W], xn_all)
            bB = streamed_group(("in", l, cfg.jj_of[KP + nS + i]), W, KD, lambda k: xn[:, k, :W], xn_all)
            bC = streamed_group(("in", l, cfg.jj_of[KP + 2 * nS + i]), W, KD, lambda k: xn[:, k, :W], xn_all)
            sx, pb, sz = nscr(), nscr(), nscr()
            S.op("act", lambda e: e.activation(out=scr[sx][:, :W], in_=ps[bC][:, :W], func=AF.Copy),
                 reads=[("ps", bC)], writes=[("scr", sx)])
            for (kind, c0, c1, b) in lay:
                S.op("dve", lambda e, c0=c0, c1=c1, b=b: e.tensor_tensor(out=scr[pb][:, b + HS:b + HS + (c1 - c0)],
                                                                        in0=ps[bB][:, c0:c1], in1=scr[sx][:, c0:c1],
                                                                        op=ALU.mult),
                     reads=[("ps", bB), ("scr", sx)], writes=[("scr", pb)])
            halo_mask(ti, pb, lay, HS)
            off = cfg.st_off("s", i)
            hist_in(pb, lay, l, off, HS)
            sw = P("sw", l) + i * KS
            for (kind, c0, c1, b) in lay:
                n = c1 - c0
                S.op("dve", lambda e, c0=c0, c1=c1, b=b, n=n: e.tensor_scalar(out=scr[sz][:, c0:c1], in0=scr[pb][:, b:b + n],
                                                                             scalar1=pcol(sw), scalar2=None, op0=ALU.mult),
                     reads=[("scr", pb), "params"], writes=[("scr", sz)])
                for k in range(1, KS):
                    S.op("dve", lambda e, c0=c0, c1=c1, b=b, n=n, k=k: e.scalar_tensor_tensor(
                        out=scr[sz][:, c0:c1], in0=scr[pb][:, b + k:b + k + n], scalar=pcol(sw + k),
                        in1=scr[sz][:, c0:c1], op0=ALU.mult, op1=ALU.add),
                        reads=[("scr", pb), ("scr", sz), "params"], writes=[("scr", sz)])
            hist_out(pb, lay, l, off, HS)
            S.op("dve", lambda e: e.tensor_tensor(out=U[:, KP + i, :W], in0=ps[bA][:, :W], in1=scr[sz][:, :W], op=ALU.mult),
                 reads=[("ps", bA), ("scr", sz)], writes=[("U", KP + i)])

        def add_to_x(bank, oc, W, stats=False):
            S.op("dve", lambda e: e.tensor_tensor(out=x[:, oc, :W], in0=ps[bank][:, :W], in1=x[:, oc, :W], op=ALU.add),
                 reads=[("ps", bank), ("x", oc)], writes=[("x", oc)])
            if stats and INC_STATS:
                q = nbf()
                S.op("act", lambda e: e.activation(out=bfs[q][:, :W], in_=x[:, oc, :W], func=AF.Square),
                     reads=[("x", oc)], writes=[("bf", q)])
                pending.append(lambda: S.op("pe", lambda e: e.matmul(ps[6][:, :W], onesD[:, :], bfs[q][:, :W],
                                                                     start=(oc == 0), stop=(oc == KD - 1)),
                                            reads=[("bf", q), "onesD"], writes=[("ps", 6)]))

        xoff = 0
        for ti, (a0, W) in enumerate(cfg.tiles):
            segs = cfg.segs(ti)
            xsz = KD * W
            S.fence_dve = len(segs) > 1
            for gq in range(XG):
                c0, c1 = gq * KD // XG, (gq + 1) * KD // XG
                S.op("sp", lambda e, xoff=xoff, W=W, c0=c0, c1=c1: [e.dma_start(
                    out=x[:, c0:c1, :W], in_=d_xt[:, xoff + c0 * W:xoff + c1 * W].rearrange("p (c w) -> p c w", w=W))],
                    writes=[("x", c) for c in range(c0, c1)], dsem="xld%d" % gq)
            for l in range(L):
                rmsnorm(W, P("gmix", l), False, have_stats=(l > 0))
                if PREFETCH:
                    prefetch_groups(W, [(("in", l, jj), KD) for jj in range(min(NSUPER, cfg.NJ))])
                dqs = []
                for (ci, sj, pc) in cfg.rounds:
                    if ci is not None:
                        cconv_head(ti, l, ci, W, segs)
                    if sj is not None:
                        sconv_head(ti, l, sj, W, segs)
                    if pc is not None:
                        dq = nbf()
                        dqs.append(dq)
                        pool_chunk(ti, l, pc, W, segs, dq)
                        if len(dqs) == PGc:
                            pending.insert(0, lambda g_=pc // PGc, dqs_=list(dqs), l=l, W=W: pool_group_mm(l, g_, W, dqs_))
                            dqs = []
                ln_done = False
                for sj in cfg.sconv_tail:
                    sconv_head(ti, l, sj, W, segs)
                    if not ln_done:
                        flush_pending()
                        ln_finish(l, W)
                        ln_done = True
                if not ln_done:
                    flush_pending()
                    ln_finish(l, W)
                if not (ti == len(cfg.tiles) - 1 and l == L - 1):
                    load_poolw((l + 1) % L)
                U_all = [("U", c) for c in range(KM)]
                auto_flush["on"] = True
                for oc in range(KD):
                    bank = streamed_group(("out", l, oc), W, KM, lambda k: U[:, k, :W], U_all)
                    add_to_x(bank, oc, W, stats=True)
                auto_flush["on"] = False
                rmsnorm(W, P("gffn", l), False, have_stats=True)
                if PREFETCH:
                    f0_, nf0_ = cfg.parts[0]
                    specs = [(("gu", l, f0_ + j, gu), KD) for j in range(min(2, nf0_)) for gu in range(2)]
                    prefetch_groups(W, specs[:NSUPER])
                for q, (f0, nf) in enumerate(cfg.parts):
                    for j in range(nf):
                        fc = f0 + j
                        bG = streamed_group(("gu", l, fc, 0), W, KD, lambda k: xn[:, k, :W], xn_all)
                        bU = streamed_group(("gu", l, fc, 1), W, KD, lambda k: xn[:, k, :W], xn_all)
                        sg = nscr()
                        S.op("act", lambda e, bG=bG, sg=sg, W=W: e.activation(out=scr[sg][:, :W], in_=ps[bG][:, :W], func=AF.Silu),
                             reads=[("ps", bG)], writes=[("scr", sg)])
                        S.op("dve", lambda e, bU=bU, sg=sg, j=j, W=W: e.tensor_tensor(out=U[:, j, :W], in0=ps[bU][:, :W],
                                                                                 in1=scr[sg][:, :W], op=ALU.mult),
                             reads=[("ps", bU), ("scr", sg)], writes=[("U", j)])
                    h_all = [("U", j) for j in range(nf)]
                    last = (q == len(cfg.parts) - 1)
                    auto_flush["on"] = last
                    for oc in range(KD):
                        bank = streamed_group(("dn", l, q, oc), W, nf, lambda k: U[:, k, :W], h_all)
                        add_to_x(bank, oc, W, stats=last)
                    auto_flush["on"] = False
            rmsnorm(W, P("gfinal"), True, have_stats=True)
            for gq in range(XG):
                c0, c1 = gq * KD // XG, (gq + 1) * KD // XG
                S.op("sp", lambda e, xoff=xoff, W=W, c0=c0, c1=c1: [e.dma_start(
                    out=d_yo[:, xoff + c0 * W:xoff + c1 * W].rearrange("p (c w) -> p c w", w=W), in_=x[:, c0:c1, :W])],
                    reads=[("x", c) for c in range(c0, c1)], dsem="yst%d" % gq)
            xoff += xsz
        assert wstate["used"] == len(wlist)

        S.op("sp", lambda e: [e.dma_start(out=d_ho[:, :], in_=hist[:, :]), e.dma_start(out=d_so[:, :], in_=hsamp[:, :])],
             reads=hkeys_m + hkeys_s + ["hist", "hsamp"], dsem="hst", ndma=2)

        def semobj(k):
            return esem[k[1]] if k[0] == "e" else dsem[k[1]]

        def run(E, name):
            for (wl, fn, ds, fence) in S.ops[name]:
                for k, v in wl:
                    E.wait_ge(semobj(k), v)
                r = fn(E)
                if ds is None:
                    r.then_inc(esem[name], 1)
                    if fence is not None:
                        E.wait_ge(esem[name], fence)
                else:
                    for ins in r:
                        ins.then_inc(dsem[ds], 16)
            if name == "sp":
                for n in ["hst"] + ["yst%d" % i for i in range(XG)]:
                    E.wait_ge(dsem[n], S.dcount[n])

        @block.tensor
        def _(e):
            run(e, "pe")

        @block.scalar
        def _(e):
            run(e, "act")

        @block.vector
        def _(e):
            run(e, "dve")

        @block.gpsimd
        def _(e):
            run(e, "pool")

        @block.sync
        def _(e):
            run(e, "sp")
    return nc


def tile_w(w, kin, nout):
    return np.ascontiguousarray(w.reshape(kin, 128, nout, 128).transpose(2, 1, 0, 3)).reshape(nout, 128, kin * 128)


def chan_cols(v):
    return np.ascontiguousarray(v.reshape(-1, 128).T)


def prep_shared(cfg, inp):
    L, KD, KM, NF = cfg.L, cfg.KD, cfg.KM, cfg.NF
    win = np.stack([tile_w(inp["w_in"][l], KD, cfg.NJ)[cfg.in_order] for l in range(L)]).reshape(L * cfg.NJ * 128, KD * 128)
    wout = np.stack([tile_w(inp["w_out"][l], KM, KD) for l in range(L)]).reshape(L * KD * 128, KM * 128)
    wgu = np.stack([np.stack([tile_w(inp["w_gate"][l], KD, NF), tile_w(inp["w_up"][l], KD, NF)], axis=1) for l in range(L)])
    wgu = wgu.reshape(L * NF * 2 * 128, KD * 128)
    sh = {"win": win, "wout": wout, "wgu": wgu}
    for q, (f0, nf) in enumerate(cfg.parts):
        sh["wdn%d" % q] = np.stack([tile_w(inp["w_down"][l][f0 * 128:(f0 + nf) * 128], nf, KD) for l in range(L)]).reshape(L * KD * 128, nf * 128)
    pw = inp["pool_w"].reshape(L, 4, cfg.PGc, 128, cfg.PG).transpose(3, 0, 1, 2, 4)
    sh["poolw"] = np.ascontiguousarray(pw).reshape(128, L * 4 * cfg.PGc * cfg.PG)
    return sh


def prep_params(cfg, inp, half):
    p = np.zeros((128, cfg.NPAR), np.float32)
    nS, nC = cfg.nS, cfg.nC
    for l in range(cfg.L):
        d = cfg.pl[l]
        p[:, d["gmix"]:d["gmix"] + cfg.KD] = chan_cols(inp["norm_mix"][l])
        p[:, d["gffn"]:d["gffn"] + cfg.KD] = chan_cols(inp["norm_ffn"][l])
        p[:, d["pscale"]:d["pscale"] + cfg.KP] = chan_cols(inp["pool_scale"][l])
        sw = inp["sconv_w"][l].reshape(KS, nS, 128).transpose(2, 1, 0).reshape(128, nS * KS)
        p[:, d["sw"]:d["sw"] + nS * KS] = sw
        cw = inp["cconv_w"][l].reshape(KC, nC, 128).transpose(2, 1, 0).reshape(128, nC * KC)
        p[:, d["cw"]:d["cw"] + nC * KC] = cw
        p[:, d["cb"]:d["cb"] + nC] = chan_cols(inp["cconv_b"][l])
        p[:, d["cg"]:d["cg"] + nC] = chan_cols(inp["cnorm_g"][l])
        p[:, d["cbt"]:d["cbt"] + nC] = chan_cols(inp["cnorm_b"][l])
    g = cfg.pg
    p[:, g["gfinal"]:g["gfinal"] + cfg.KD] = chan_cols(inp["norm_final"])
    p[:, g["flag"]] = float(half)
    p[:, g["eps_rms"]] = RMS_EPS
    p[:, g["eps_ln"]] = LN_EPS
    for gi, k in enumerate(POOL_WINDOWS):
        for j in range(16):
            cnt = min(k, j + 1) if half == 0 else k
            p[:, g["tbl"] + gi * 16 + j] = np.float32(1.0) / np.float32(cnt)
    return p


def prep_hsin(cfg, inp, sb):
    h = np.zeros((128, cfg.L, cfg.SW), np.float32)
    for l in range(cfg.L):
        cp = inp["cache_pool"][l, sb]
        h[:, l, 0:cfg.KP * HP] = cp.reshape(HP, cfg.KP, 128).transpose(2, 1, 0).reshape(128, cfg.KP * HP)
        o = cfg.KP * HP
        cs = inp["cache_sconv"][l, sb]
        h[:, l, o:o + cfg.nS * HS] = cs.reshape(HS, cfg.nS, 128).transpose(2, 1, 0).reshape(128, cfg.nS * HS)
        o += cfg.nS * HS
        cc = inp["cache_cconv"][l, sb]
        h[:, l, o:o + cfg.nC * HC] = cc.reshape(HC, cfg.nC, 128).transpose(2, 1, 0).reshape(128, cfg.nC * HC)
    return h.reshape(128, cfg.L * cfg.SW)


def prep_x(cfg, xcols):
    out = []
    for (a, w) in cfg.tiles:
        blk = xcols[a:a + w].T.reshape(cfg.KD, 128, w).transpose(1, 0, 2).reshape(128, cfg.KD * w)
        out.append(blk)
    return np.ascontiguousarray(np.concatenate(out, axis=1))


def unprep_y(cfg, yo):
    rows = []
    off = 0
    for (a, w) in cfg.tiles:
        blk = yo[:, off:off + cfg.KD * w].reshape(128, cfg.KD, w).transpose(2, 1, 0).reshape(w, cfg.D)
        rows.append(blk)
        off += cfg.KD * w
    return np.concatenate(rows, axis=0)


def unprep_state(cfg, st):
    st = st.reshape(128, cfg.L, cfg.SW)
    pools, sconvs, cconvs = [], [], []
    for l in range(cfg.L):
        o = 0
        pools.append(st[:, l, o:o + cfg.KP * HP].reshape(128, cfg.KP, HP).transpose(2, 1, 0).reshape(HP, cfg.KP * 128))
        o += cfg.KP * HP
        sconvs.append(st[:, l, o:o + cfg.nS * HS].reshape(128, cfg.nS, HS).transpose(2, 1, 0).reshape(HS, cfg.nS * 128))
        o += cfg.nS * HS
        cconvs.append(st[:, l, o:o + cfg.nC * HC].reshape(128, cfg.nC, HC).transpose(2, 1, 0).reshape(HC, cfg.nC * 128))
    return np.stack(pools), np.stack(sconvs), np.stack(cconvs)


def run_cfg(cfg, inp, n_cores=8):
    inp = {k: np.asarray(v) for k, v in inp.items()}
    B = inp["x_prompt"].shape[0]
    assert 2 * B == n_cores and inp["x_sample"].shape[0] == n_cores
    Lm, D = cfg.Lm, cfg.D
    shared = prep_shared(cfg, inp)
    in_maps = []
    for core in range(n_cores):
        b, half = core // 2, core % 2
        xc = np.zeros((cfg.T, D), np.float32)
        if half == 1:
            xc[:HALO] = inp["x_prompt"][b, Lm - HALO:Lm]
        xc[HALO:HALO + Lm] = inp["x_prompt"][b, half * Lm:(half + 1) * Lm]
        xc[HALO + Lm:] = inp["x_sample"][core]
        m = dict(shared)
        m["xt"] = prep_x(cfg, xc)
        m["params"] = prep_params(cfg, inp, half)
        m["hsin"] = prep_hsin(cfg, inp, core)
        in_maps.append(m)
    nc = build_program(cfg)
    res = run_bass_kernel_spmd(nc, in_maps, core_ids=list(range(n_cores)))
    S_, L = 2 * Lm, cfg.L
    y_prompt = np.zeros((B, S_, D), np.float32)
    y_sample = np.zeros((n_cores, cfg.Ls, D), np.float32)
    DP, DS, DC = cfg.KP * 128, cfg.nS * 128, cfg.nC * 128
    pool_p = np.zeros((L, B, HP, DP), np.float32)
    pool_s = np.zeros((L, n_cores, HP, DP), np.float32)
    sc_p = np.zeros((L, B, HS, DS), np.float32)
    sc_s = np.zeros((L, n_cores, HS, DS), np.float32)
    cc_p = np.zeros((L, B, HC, DC), np.float32)
    cc_s = np.zeros((L, n_cores, HC, DC), np.float32)
    for core in range(n_cores):
        b, half = core // 2, core % 2
        r = res.results[core]
        Y = unprep_y(cfg, np.asarray(r["yo"]))
        y_prompt[b, half * Lm:(half + 1) * Lm] = Y[HALO:HALO + Lm]
        y_sample[core] = Y[HALO + Lm:]
        p, s, c = unprep_state(cfg, np.asarray(r["so"]))
        pool_s[:, core], sc_s[:, core], cc_s[:, core] = p, s, c
        if half == 1:
            p, s, c = unprep_state(cfg, np.asarray(r["ho"]))
            pool_p[:, b], sc_p[:, b], cc_p[:, b] = p, s, c
    return (y_prompt, y_sample, pool_p, pool_s, sc_p, sc_s, cc_p, cc_s)


def kernel(**inputs):
    return run_cfg(FULL, inputs)
```
